# Optimizing a Trainium2 kernel written in Bass

```python
import math
import jax, jax.numpy as jnp
from jax import lax
import numpy as np

D_MODEL = 1024
BATCH = 16
SEQ = 2048
DEPTH = 1

CTX_LEN = 256
GRID_W = 64
RET_HEADS = 4
RET_DK = 64
RET_DV = 128
RET_CHUNK = 128
MLA_HEADS = 4
MLA_NOPE = 128
MLA_ROPE = 64
MLA_V = 128
Q_LORA = 384
KV_LORA = 256
D_MIX = RET_HEADS * RET_DV + MLA_HEADS * MLA_V
D_FF = 4 * D_MODEL
ROPE_BASE = 10000.0
Q_BLOCK = 128
EPS = 1e-6
IN_SPLITS = (RET_HEADS * RET_DK, RET_HEADS * RET_DK, RET_HEADS * RET_DV, RET_HEADS * RET_DV,
             Q_LORA, KV_LORA, MLA_ROPE)
IN_COLS = sum(IN_SPLITS)
SPLIT_POINTS = tuple(int(v) for v in np.cumsum(IN_SPLITS)[:-1])

kernel_name = "hymba_retention_mla_adaln_prefix_block"


def rms_norm(x, g):
    x32 = x.astype(jnp.float32)
    y = x32 * lax.rsqrt(jnp.mean(x32 * x32, axis=-1, keepdims=True) + EPS)
    return (y * g.astype(jnp.float32)).astype(x.dtype)


def modulate(h, shift, scale):
    return h * (1 + scale) + shift


def axial_rope_tables(rows, dim):
    row = jnp.repeat(jnp.arange(rows, dtype=jnp.float32), GRID_W)
    col = jnp.tile(jnp.arange(GRID_W, dtype=jnp.float32), rows)
    n_freq = dim // 4
    freq = ROPE_BASE ** (-jnp.arange(n_freq, dtype=jnp.float32) / n_freq)
    ang = jnp.concatenate([row[:, None] * freq, col[:, None] * freq], axis=-1)
    return jnp.cos(ang)[:, None, :], jnp.sin(ang)[:, None, :]


def apply_rope(x, cos, sin):
    half = x.shape[-1] // 2
    x1, x2 = x[..., :half], x[..., half:]
    return jnp.concatenate([x1 * cos - x2 * sin, x2 * cos + x1 * sin], axis=-1).astype(x.dtype)


def retention_chunked(q, k, v, log_gamma, s0):
    B, L, H, dk = q.shape
    dv = v.shape[-1]
    n = L // RET_CHUNK
    qc = q.reshape(B, n, RET_CHUNK, H, dk)
    kc = k.reshape(B, n, RET_CHUNK, H, dk)
    vc = v.reshape(B, n, RET_CHUNK, H, dv)
    pos = jnp.arange(RET_CHUNK, dtype=jnp.float32)
    diff = pos[:, None] - pos[None, :]
    dec = jnp.where(diff >= 0, jnp.exp(log_gamma[:, None, None] * jnp.maximum(diff, 0.0)), 0.0)
    scores = jnp.einsum('bnihd,bnjhd->bnhij', qc, kc) * dec
    o_intra = jnp.einsum('bnhij,bnjhe->bnihe', scores, vc)
    w_k = jnp.exp(log_gamma[:, None] * (RET_CHUNK - 1 - pos)[None, :])
    kv = jnp.einsum('bnjhd,hj,bnjhe->bnhde', kc, w_k, vc).astype(jnp.float32)
    chunk_decay = jnp.exp(log_gamma * RET_CHUNK)[:, None, None]

    def step(s, kv_n):
        return chunk_decay * s + kv_n, s

    _, s_prev = lax.scan(step, s0.astype(jnp.float32), jnp.moveaxis(kv, 1, 0))
    w_q = jnp.exp(log_gamma[:, None] * (pos + 1.0)[None, :])
    o_cross = jnp.einsum('bnihd,hi,nbhde->bnihe', qc, w_q, s_prev)
    return (o_intra + o_cross).reshape(B, L, H, dv)


def retention_final_state(k, v, log_gamma):
    L = k.shape[1]
    w = jnp.exp(log_gamma[:, None] * (L - 1 - jnp.arange(L, dtype=jnp.float32))[None, :])
    return jnp.einsum('blhd,hl,blhe->bhde', k, w, v).astype(jnp.float32)


def retention_mix(q, k, v, gate, lg_f, lg_b, g_ret, s_f, s_b):
    B, L, H, dv = v.shape
    o_f = retention_chunked(q, k, v, lg_f, s_f)
    o_b = retention_chunked(q[:, ::-1], k[:, ::-1], v[:, ::-1], lg_b, s_b)[:, ::-1]
    o = (o_f + o_b).astype(jnp.float32)
    mu = jnp.mean(o, axis=-1, keepdims=True)
    var = jnp.mean(jnp.square(o - mu), axis=-1, keepdims=True)
    o = (o - mu) * lax.rsqrt(var + EPS) * g_ret.astype(jnp.float32).reshape(H, dv)
    return (o.reshape(B, L, H * dv) * jax.nn.silu(gate.astype(jnp.float32))).astype(gate.dtype)


def attend(q, k, v):
    s = jnp.einsum('bqhd,bkhd->bhqk', q, k).astype(jnp.float32) * (1.0 / math.sqrt(q.shape[-1]))
    p = jax.nn.softmax(s, axis=-1)
    return jnp.einsum('bhqk,bkhe->bqhe', p.astype(v.dtype), v)


def blocked_attention(q, k, v):
    B, L, H, d = q.shape
    qb = q.reshape(B, L // Q_BLOCK, Q_BLOCK, H, d).swapaxes(0, 1)
    out = lax.map(lambda qi: attend(qi, k, v), qb)
    return out.swapaxes(0, 1).reshape(B, L, H, v.shape[-1])


def head_group_inputs(h, w_in, g_q, w_uq, g_kv, w_ukv):
    B, L, _ = h.shape
    r_q, r_k, r_v, r_g, c_q, c_kv, k_pe = jnp.split(h @ w_in, SPLIT_POINTS, axis=-1)
    r_q = r_q.reshape(B, L, RET_HEADS, RET_DK)
    r_k = r_k.reshape(B, L, RET_HEADS, RET_DK) * (RET_DK ** -0.5)
    r_v = r_v.reshape(B, L, RET_HEADS, RET_DV)
    q = (rms_norm(c_q, g_q) @ w_uq).reshape(B, L, MLA_HEADS, MLA_NOPE + MLA_ROPE)
    kv = (rms_norm(c_kv, g_kv) @ w_ukv).reshape(B, L, MLA_HEADS, MLA_NOPE + MLA_V)
    q_nope, q_pe = q[..., :MLA_NOPE], q[..., MLA_NOPE:]
    k_nope, m_v = kv[..., :MLA_NOPE], kv[..., MLA_NOPE:]
    k_pe = k_pe[:, :, None, :]
    return r_q, r_k, r_v, r_g, q_nope, q_pe, k_nope, k_pe, m_v


def mla_qk(q_nope, q_pe, k_nope, k_pe):
    q = jnp.concatenate([q_nope, q_pe], axis=-1)
    k = jnp.concatenate([k_nope, jnp.broadcast_to(k_pe, k_nope.shape[:-1] + (MLA_ROPE,))], axis=-1)
    return q, k


def sq_relu_mlp(h, w1, w2):
    return jnp.square(jax.nn.relu(h @ w1)) @ w2


def setup_inputs(seed: int = 0) -> dict:
    key = jax.random.key(seed)
    ks = jax.random.split(key, 24)
    f32 = jnp.float32

    def nrm(k, shape, scale):
        return jax.random.normal(k, shape, f32) * scale

    base_logit = jnp.log(2.0 ** (5.0 + jnp.arange(RET_HEADS, dtype=f32)) - 1.0)
    return {
        "x": nrm(ks[0], (BATCH, SEQ, D_MODEL), 1.0),
        "c": nrm(ks[1], (BATCH, D_MODEL), 1.0),
        "ctx": nrm(ks[2], (BATCH, CTX_LEN, D_MODEL), 1.0),
        "c_ctx": nrm(ks[3], (D_MODEL,), 1.0),
        "w_ada": nrm(ks[4], (DEPTH, D_MODEL, 6 * D_MODEL), 0.5 * D_MODEL ** -0.5),
        "b_ada": nrm(ks[5], (DEPTH, 6 * D_MODEL), 0.01),
        "g_attn": 1.0 + nrm(ks[6], (DEPTH, D_MODEL), 0.05),
        "g_ffn": 1.0 + nrm(ks[7], (DEPTH, D_MODEL), 0.05),
        "w_in": nrm(ks[8], (DEPTH, D_MODEL, IN_COLS), D_MODEL ** -0.5),
        "ret_decay_fwd": base_logit + nrm(ks[9], (DEPTH, RET_HEADS), 0.1),
        "ret_decay_bwd": base_logit + nrm(ks[10], (DEPTH, RET_HEADS), 0.1),
        "g_ret": 1.0 + nrm(ks[11], (DEPTH, RET_HEADS * RET_DV), 0.05),
        "g_q_lora": 1.0 + nrm(ks[12], (DEPTH, Q_LORA), 0.05),
        "w_uq": nrm(ks[13], (DEPTH, Q_LORA, MLA_HEADS * (MLA_NOPE + MLA_ROPE)), Q_LORA ** -0.5),
        "g_kv_lora": 1.0 + nrm(ks[14], (DEPTH, KV_LORA), 0.05),
        "w_ukv": nrm(ks[15], (DEPTH, KV_LORA, MLA_HEADS * (MLA_NOPE + MLA_V)), KV_LORA ** -0.5),
        "w_out": nrm(ks[16], (DEPTH, D_MIX, D_MODEL), D_MIX ** -0.5),
        "w_ff1": nrm(ks[17], (DEPTH, D_MODEL, D_FF), D_MODEL ** -0.5),
        "w_ff2": nrm(ks[18], (DEPTH, D_FF, D_MODEL), D_FF ** -0.5),
        "g_final": 1.0 + nrm(ks[19], (D_MODEL,), 0.05),
    }


def reference(x, c, ctx, c_ctx, w_ada, b_ada, g_attn, g_ffn, w_in, ret_decay_fwd, ret_decay_bwd,
              g_ret, g_q_lora, w_uq, g_kv_lora, w_ukv, w_out, w_ff1, w_ff2, g_final):
    B, L, _ = x.shape
    rows = L // GRID_W
    cos, sin = axial_rope_tables(rows, RET_DK)
    for l in range(DEPTH):
        mod = jax.nn.silu(c) @ w_ada[l] + b_ada[l]
        mod_c = jax.nn.silu(c_ctx) @ w_ada[l] + b_ada[l]
        sh_a, sc_a, gt_a, sh_f, sc_f, gt_f = [m[:, None, :] for m in jnp.split(mod, 6, axis=-1)]
        csh_a, csc_a, cgt_a, csh_f, csc_f, cgt_f = jnp.split(mod_c, 6, axis=-1)

        h = modulate(rms_norm(x, g_attn[l]), sh_a, sc_a)
        hc = modulate(rms_norm(ctx, g_attn[l]), csh_a, csc_a)
        rq, rk, rv, rg, qn, qp, kn, kp, mv = head_group_inputs(h, w_in[l], g_q_lora[l], w_uq[l],
                                                                g_kv_lora[l], w_ukv[l])
        rqc, rkc, rvc, rgc, qnc, qpc, knc, kpc, mvc = head_group_inputs(hc, w_in[l], g_q_lora[l], w_uq[l],
                                                                        g_kv_lora[l], w_ukv[l])
        rq, rk = apply_rope(rq, cos, sin), apply_rope(rk, cos, sin)
        qp, kp = apply_rope(qp, cos, sin), apply_rope(kp, cos, sin)

        lg_f = jax.nn.log_sigmoid(ret_decay_fwd[l].astype(jnp.float32))
        lg_b = jax.nn.log_sigmoid(ret_decay_bwd[l].astype(jnp.float32))
        s_f = retention_final_state(rkc, rvc, lg_f)
        s_b = retention_final_state(rkc[:, ::-1], rvc[:, ::-1], lg_b)
        y_ret = retention_mix(rq, rk, rv, rg, lg_f, lg_b, g_ret[l], s_f, s_b)

        q_m, k_m = mla_qk(qn, qp, kn, kp)
        q_mc, k_mc = mla_qk(qnc, qpc, knc, kpc)
        y_mla = blocked_attention(q_m, jnp.concatenate([k_mc, k_m], axis=1),
                                  jnp.concatenate([mvc, mv], axis=1)).reshape(B, L, MLA_HEADS * MLA_V)

        x_mid = x + gt_a * (jnp.concatenate([y_ret, y_mla], axis=-1) @ w_out[l])

        if l + 1 < DEPTH:
            zero_state = jnp.zeros((B, RET_HEADS, RET_DK, RET_DV), jnp.float32)
            y_ret_c = retention_mix(rqc, rkc, rvc, rgc, lg_f, lg_b, g_ret[l], zero_state, zero_state)
            y_mla_c = attend(q_mc, k_mc, mvc).reshape(B, CTX_LEN, MLA_HEADS * MLA_V)
            ctx = ctx + cgt_a * (jnp.concatenate([y_ret_c, y_mla_c], axis=-1) @ w_out[l])
            ctx = ctx + cgt_f * sq_relu_mlp(modulate(rms_norm(ctx, g_ffn[l]), csh_f, csc_f),
                                            w_ff1[l], w_ff2[l])

        x = x_mid + gt_f * sq_relu_mlp(modulate(rms_norm(x_mid, g_ffn[l]), sh_f, sc_f),
                                       w_ff1[l], w_ff2[l])
    return rms_norm(x, g_final)
```

```python
import contextlib
import numpy as np
import concourse.bass as bass
import concourse.mybir as mybir
from concourse.bass_utils import run_bass_kernel_spmd

DT = mybir.dt
AF = mybir.ActivationFunctionType
ALU = mybir.AluOpType
F32 = DT.float32
BF16 = DT.bfloat16
I32 = DT.int32

_ES = {}


def _esize(dtype):
    if dtype not in _ES:
        _ES[dtype] = {"float32": 4, "bfloat16": 2, "int32": 4, "uint32": 4, "float16": 2, "uint8": 1,
                      "int8": 1, "uint16": 2, "int16": 2, "float32r": 4}[str(dtype).split(".")[-1]]
    return _ES[dtype]


class _Op:
    __slots__ = ("id", "eng", "fn", "deps", "is_dma", "queue", "sem", "semval", "count", "signal",
                 "eidx", "snap", "waits", "is_out")


class Sched:
    ENGINES = ("pe", "act", "dve", "pool", "sp")
    SAME_ENGINE_WINDOW = 4

    def __init__(self, nc, sbuf_limit=None):
        self.nc = nc
        self.ops = []
        self.base = {}
        self.sb_off = 16640
        self.sb_limit = sbuf_limit or 229376
        self.wlive = []
        self.rlive = []
        self.n_psum = 0

    def sbuf(self, name, shape, dtype, at=None):
        nbytes = int(np.prod(shape[1:])) * _esize(dtype)
        if at is None:
            off = (self.sb_off + 63) // 64 * 64
            self.sb_off = off + nbytes
            assert self.sb_off <= self.sb_limit, f"SBUF overflow at {name}: {self.sb_off} > {self.sb_limit}"
        else:
            off = at
        t = self.nc.alloc_sbuf_tensor_at(name, list(shape), dtype, offset=off)
        ap = t.ap()
        self.base[ap.name] = ("SB", off)
        return ap

    def psum(self, name, bank):
        t = self.nc.alloc_psum_tensor(name, [128, 512], F32)
        ap = t.ap()
        self.base[ap.name] = ("PS", bank * 2048)
        return ap

    def _region(self, ap):
        ent = self.base.get(ap.name)
        if ent is None:
            return None
        space, base = ent
        es = _esize(ap.dtype)
        pat = ap.ap
        pstep, pcount = pat[0]
        off = ap.offset
        if pstep > 0:
            p0 = off // pstep
            fo = off % pstep
        else:
            p0 = ap.base_partition()
            fo = off
        lo = fo
        hi = fo
        for st, cnt in pat[1:]:
            ext = st * (cnt - 1)
            if ext < 0:
                lo += ext
            else:
                hi += ext
        if space == "PS":
            return (space, base, base + 2048, 0, 128)
        return (space, base + lo * es, base + (hi + 1) * es, p0, p0 + pcount)

    @staticmethod
    def _ov(a, b):
        return a[0] == b[0] and a[1] < b[2] and b[1] < a[2] and a[3] < b[4] and b[3] < a[4]

    @staticmethod
    def _contains(outer, inner):
        return (outer[0] == inner[0] and outer[1] <= inner[1] and inner[2] <= outer[2]
                and outer[3] <= inner[3] and inner[4] <= outer[4])

    def op(self, eng, fn, r=(), w=(), _dma=False, _out=False):
        o = _Op()
        o.id = len(self.ops)
        o.eng = eng
        o.fn = fn
        o.is_dma = _dma
        o.is_out = _out
        o.signal = False
        o.count = 0
        rr = [x for x in (self._region(a) for a in r) if x is not None]
        ww = [x for x in (self._region(a) for a in w) if x is not None]
        deps = set()
        for rec in self.wlive:
            for reg in rr:
                if self._ov(rec, reg):
                    deps.add(rec[5])
                    break
            else:
                for reg in ww:
                    if self._ov(rec, reg):
                        deps.add(rec[5])
                        break
        for rec in self.rlive:
            for reg in ww:
                if self._ov(rec, reg):
                    deps.add(rec[5])
                    break
        deps.discard(o.id)
        o.deps = deps
        for d in deps:
            self.ops[d].signal = True
        if ww:
            self.wlive = [rec for rec in self.wlive if not any(self._contains(reg, rec) for reg in ww)]
            self.rlive = [rec for rec in self.rlive if not any(self._contains(reg, rec) for reg in ww)]
            for reg in ww:
                self.wlive.append(reg + (o.id,))
        if rr:
            if not _dma:
                self.rlive = [rec for rec in self.rlive
                              if not (self.ops[rec[5]].eng == eng and not self.ops[rec[5]].is_dma
                                      and any(self._contains(reg, rec) for reg in rr))]
            for reg in rr:
                self.rlive.append(reg + (o.id,))
        self.ops.append(o)
        return o

    def dma(self, queue, out_ap, in_ap, out=False, **kw):
        def fn(e, out_ap=out_ap, in_ap=in_ap, kw=kw):
            return e.dma_start(out=out_ap, in_=in_ap, **kw)
        o = self.op(queue, fn, r=[in_ap], w=[out_ap], _dma=True, _out=out)
        if out:
            o.signal = True
        return o

    def emit(self, n_dma_sems=None):
        nc = self.nc
        n_dma_sems = n_dma_sems or {"sp": 14, "pool": 10, "act": 4}
        with contextlib.ExitStack() as st:
            esem = {e: st.enter_context(nc.semaphore(f"s_{e}")) for e in ("pe", "act", "dve", "pool")}
            dsem = {q: [st.enter_context(nc.semaphore(f"d_{q}{i}")) for i in range(n)]
                    for q, n in n_dma_sems.items()}
            dcnt = {q: [0] * n for q, n in n_dma_sems.items()}
            drr = {q: 0 for q in n_dma_sems}
            known = {e: {} for e in self.ENGINES}
            ecount = {e: 0 for e in self.ENGINES}
            eidx = {e: 0 for e in self.ENGINES}
            per_eng = {e: [] for e in self.ENGINES}
            out_waits = {}
            for o in self.ops:
                E = o.eng
                kn = known[E]
                o.eidx = eidx[E]
                eidx[E] += 1
                waits = {}
                for d in o.deps:
                    dop = self.ops[d]
                    if dop.is_dma:
                        sem, val = dop.sem, dop.semval
                    else:
                        if dop.eng == E:
                            if E == "pe" or o.eidx - dop.eidx > self.SAME_ENGINE_WINDOW:
                                continue
                        sem, val = esem[dop.eng], dop.count
                    if kn.get(sem, 0) >= val:
                        continue
                    if waits.get(sem, (0, None))[0] < val:
                        waits[sem] = (val, dop)
                if o.is_dma:
                    i = drr[E]
                    drr[E] = (i + 1) % len(dsem[E])
                    o.sem = dsem[E][i]
                    prev = dcnt[E][i]
                    if prev > 0 and kn.get(o.sem, 0) < prev and waits.get(o.sem, (0, None))[0] < prev:
                        waits[o.sem] = (prev, None)
                    dcnt[E][i] = prev + 16
                    o.semval = prev + 16
                    out_waits[o.sem] = o.semval
                elif o.signal:
                    ecount[E] += 1
                    o.count = ecount[E]
                for sem, (val, dop) in waits.items():
                    if kn.get(sem, 0) < val:
                        kn[sem] = val
                    if dop is not None and dop.snap is not None:
                        for s2, v2 in dop.snap.items():
                            if kn.get(s2, 0) < v2:
                                kn[s2] = v2
                o.waits = [(sem, val) for sem, (val, _) in waits.items()]
                o.snap = dict(kn) if o.signal else None
                if o.signal and not o.is_dma:
                    o.snap[esem[E]] = o.count
                per_eng[E].append(o)
            self.stats = {e: (len(per_eng[e]), sum(len(o.waits) for o in per_eng[e])) for e in self.ENGINES}

            def run(eng_name, e):
                for o in per_eng[eng_name]:
                    for sem, val in o.waits:
                        e.wait_ge(sem, val)
                    ins = o.fn(e)
                    if o.is_dma:
                        ins.then_inc(o.sem, 16)
                    elif o.signal:
                        ins.then_inc(esem[eng_name], 1)
                if eng_name == "sp":
                    for sem, val in out_waits.items():
                        e.wait_ge(sem, val)

            with nc.Block() as block:
                @block.tensor
                def _(e):
                    run("pe", e)

                @block.scalar
                def _(e):
                    run("act", e)

                @block.vector
                def _(e):
                    run("dve", e)

                @block.gpsimd
                def _(e):
                    run("pool", e)

                @block.sync
                def _(e):
                    run("sp", e)


P = 128
D = 1024
L = 2048
NT = 16
NCH = 18
DFF = 4096
EPS = 1e-6
PI = float(np.pi)


class Ring:
    def __init__(self, items):
        self.items = list(items)
        self.i = 0

    def next(self):
        x = self.items[self.i % len(self.items)]
        self.i += 1
        return x


def v3(ap, a, b):
    return ap.rearrange("p (a b) -> p a b", a=a, b=b)


class B_:
    def __init__(self, S):
        self.S = S

    def mm(self, out, lhsT, rhs, start=True, stop=True):
        self.S.op("pe", lambda e: e.matmul(out, lhsT, rhs, start=start, stop=stop), r=[lhsT, rhs], w=[out])

    def tr(self, out, in_, ident):
        self.S.op("pe", lambda e: e.transpose(out, in_, ident), r=[in_, ident], w=[out])

    def act(self, out, in_, func, scale=1.0, bias=0.0, accum=None):
        r = [in_] + [a for a in (scale, bias) if not isinstance(a, (int, float))]
        w = [out] + ([accum] if accum is not None else [])
        if accum is not None:
            self.S.op("act", lambda e: e.activation(out, in_, func, bias=bias, scale=scale, accum_out=accum), r=r, w=w)
        else:
            self.S.op("act", lambda e: e.activation(out, in_, func, bias=bias, scale=scale), r=r, w=w)

    def cp(self, eng, out, in_):
        if eng == "act":
            self.S.op("act", lambda e: e.activation(out, in_, AF.Copy), r=[in_], w=[out])
        else:
            self.S.op(eng, lambda e: e.tensor_copy(out, in_), r=[in_], w=[out])

    def tt(self, eng, out, in0, in1, op):
        self.S.op(eng, lambda e: e.tensor_tensor(out, in0, in1, op), r=[in0, in1], w=[out])

    def ts(self, eng, out, in0, s1, s2=None, op0=ALU.mult, op1=None):
        r = [in0] + [a for a in (s1, s2) if a is not None and not isinstance(a, (int, float))]
        if op1 is None:
            self.S.op(eng, lambda e: e.tensor_scalar(out, in0, s1, s2, op0), r=r, w=[out])
        else:
            self.S.op(eng, lambda e: e.tensor_scalar(out, in0, s1, s2, op0, op1), r=r, w=[out])

    def stt(self, out, in0, scalar, in1, op0, op1):
        r = [in0, in1] + ([] if isinstance(scalar, (int, float)) else [scalar])
        self.S.op("dve", lambda e: e.scalar_tensor_tensor(out, in0, scalar, in1, op0, op1), r=r, w=[out])

    def memset(self, eng, out, val):
        self.S.op(eng, lambda e: e.memset(out, val), w=[out])

    def iota(self, out, pattern, base, cm):
        self.S.op("pool", lambda e: e.iota(out, pattern, base=base, channel_multiplier=cm), w=[out])


class _Stop(Exception):
    pass


def build(nb=2, dbg=(), stop_after=None):
    nc = bass.Bass("TRN2", target_bir_lowering=False)
    dbg_d = {}
    try:
        _build(nc, nb, dbg, stop_after, dbg_d)
    except _Stop:
        pass
    return nc, dbg_d


def _build(nc, nb, dbg, stop_after, dbg_d):

    def dram(name, shape, kind="ExternalInput", dt=F32):
        return nc.dram_tensor(name, list(shape), dt, kind=kind).ap()

    x_d = dram("x", [nb, L, D]); c_d = dram("c", [nb, D]); ctx_d = dram("ctx", [nb, 256, D]); cctx_d = dram("c_ctx", [D])
    wada_d = dram("w_ada", [D, 6 * D]); bada_d = dram("b_ada", [6 * D])
    gattn_d = dram("g_attn", [D]); gffn_d = dram("g_ffn", [D]); win_d = dram("w_in", [D, 2240])
    decf_d = dram("ret_decay_fwd", [4]); decb_d = dram("ret_decay_bwd", [4]); gret_d = dram("g_ret", [512])
    gq_d = dram("g_q_lora", [384]); wuq_d = dram("w_uq", [384, 768]); gkv_d = dram("g_kv_lora", [256])
    wukv_d = dram("w_ukv", [256, 1024]); wout_d = dram("w_out", [D, D]); wff1_d = dram("w_ff1", [D, DFF])
    wff2_d = dram("w_ff2", [DFF, D]); gfin_d = dram("g_final", [D])
    out_d = dram("out", [nb, L, D], kind="ExternalOutput")

    def checkpoint(tag):
        if stop_after == tag:
            S.emit()
            raise _Stop()

    S = Sched(nc)
    Bx = B_(S)
    mm, tr, act, cp, tt, ts, stt = Bx.mm, Bx.tr, Bx.act, Bx.cp, Bx.tt, Bx.ts, Bx.stt

    def dbg_dump(name, ap, shape, dt=F32):
        if name in dbg:
            d = dram("dbg_" + name, shape, kind="ExternalOutput", dt=dt)
            dbg_d[name] = d
            S.dma("sp", d, ap, out=True)

    ident_f = S.sbuf("ident_f", [P, P], F32)
    ident_b = S.sbuf("ident_b", [P, P], BF16)
    ones_f = S.sbuf("ones_f", [P, P], F32)
    ones_b = S.sbuf("ones_b", [P, P], BF16)
    Df = S.sbuf("Df", [P, P], F32)
    colsT = S.sbuf("colsT", [P, 96], F32)
    modT = S.sbuf("modT", [P, 48, 3], F32)
    Gall = S.sbuf("Gall", [P, 3, 2, 8], F32)
    scT = S.sbuf("scT", [P, 8, 3], BF16)
    gfin_row = S.sbuf("gfin_row", [P, D], F32)
    gret_row = S.sbuf("gret_row", [P, 512], F32)
    gq_row = S.sbuf("gq_row", [P, 384], F32)
    gkv_row = S.sbuf("gkv_row", [P, 256], F32)
    GTa = S.sbuf("GTa", [P, D], F32)
    GTf = S.sbuf("GTf", [P, D], F32)
    cosT = S.sbuf("cosT", [P, NT, 32], F32)
    sinT = S.sbuf("sinT", [P, NT, 32], F32)
    nsinT = S.sbuf("nsinT", [P, NT, 32], F32)
    maskT = S.sbuf("maskT", [P, 4, P], F32)
    WQ = S.sbuf("WQ", [P, 4, P], F32)
    gam = S.sbuf("gam", [P, 8], F32)
    gcol = S.sbuf("gcol", [P, 4], F32)
    cdc = S.sbuf("cdc", [P, 4], F32)
    wk = S.sbuf("wk", [P, 8], F32)
    mhalf = S.sbuf("mhalf", [P, 4], F32)
    npi = S.sbuf("npi", [P, 1], F32)
    stat = S.sbuf("stat", [P, 256], F32)
    YT = S.sbuf("YT", [P, 8, L], BF16)
    gdiag = Ring([S.sbuf(f"gdiag{i}", [P, P], F32) for i in range(2)])
    stat_i = [0]

    def st(n=1):
        i = stat_i[0]
        if i + n > 256:
            i = 0
        stat_i[0] = i + n
        return stat[:, i:i + n]

    ARENA = (S.sb_off + 63) // 64 * 64
    ARENA_END = S.sb_limit

    class Lay:
        def __init__(self, tag, start=None):
            self.off = ARENA if start is None else start
            self.tag = tag

        def a(self, name, shape, dtype):
            nbytes = int(np.prod(shape[1:])) * _esize(dtype)
            off = (self.off + 63) // 64 * 64
            self.off = off + nbytes
            assert self.off <= ARENA_END, f"arena overflow {self.tag}:{name} {self.off - ARENA_END}"
            return S.sbuf(f"{self.tag}_{name}", shape, dtype, at=off)

    banks = [S.psum(f"bank{i}", i) for i in range(8)]

    def bfv(bank):
        return bank.bitcast(BF16)

    LS = Lay("S")
    rows = LS.a("rows", [P, P], F32)
    iot = LS.a("iot", [P, P], I32)
    tmpA = LS.a("tmpA", [P, 512], F32)
    tmpB = LS.a("tmpB", [P, 512], F32)
    tmpC = LS.a("tmpC", [P, 512], F32)
    tmpI = LS.a("tmpI", [P, 512], I32)
    ang = LS.a("ang", [P, NT, 32], F32)
    E1 = LS.a("E1", [P, P], F32)
    E2 = LS.a("E2", [P, P], F32)
    EK = LS.a("EK", [P, 8], F32)
    c128 = LS.a("c128", [P, 4], F32)
    dec = LS.a("dec", [P, 8], F32)
    sm = LS.a("sm", [P, 64], F32)
    wada_s = [LS.a(f"wada{i}", [P, 8, 1024], BF16) for i in range(2)]

    Bx.iota(iot[:], [[1, P]], 0, -1)
    cp("dve", Df[:], iot[:])
    ts("dve", ident_f[:], Df[:], 0.0, None, ALU.is_equal)
    cp("dve", ident_b[:], ident_f[:])
    Bx.memset("pool", ones_f[:], 1.0)
    Bx.memset("pool", ones_b[:], 1.0)
    Bx.memset("pool", mhalf[:], -0.5)
    Bx.memset("pool", npi[:], -PI)
    Bx.memset("pool", c128[:], 128.0)
    Bx.memset("dve", rows[:], 0.0)

    def rowsrc(v, n):
        return v.rearrange("(r f) -> r f", f=P)
    S.dma("sp", rows[0:48, :], rowsrc(bada_d, 48))
    S.dma("sp", rows[48:56, :], rowsrc(gattn_d, 8))
    S.dma("sp", rows[56:64, :], rowsrc(gffn_d, 8))
    S.dma("sp", rows[64:67, :], rowsrc(gq_d, 3))
    S.dma("sp", rows[67:69, :], rowsrc(gkv_d, 2))
    for b in range(nb):
        S.dma("sp", rows[69 + 8 * b:77 + 8 * b, :], rowsrc(c_d[b], 8))
    S.dma("sp", rows[85:93, :], rowsrc(cctx_d, 8))
    S.dma("sp", gfin_row[:], gfin_d.partition_broadcast(P))
    S.dma("sp", gret_row[:], gret_d.partition_broadcast(P))
    S.dma("sp", gq_row[:], gq_d.partition_broadcast(P))
    S.dma("sp", gkv_row[:], gkv_d.partition_broadcast(P))
    S.dma("sp", dec[:, 0:4], decf_d.partition_broadcast(P))
    S.dma("sp", dec[:, 4:8], decb_d.partition_broadcast(P))

    tr(banks[0][:, 0:P], rows[:, :], ident_f[:])
    cp("dve", colsT[:], banks[0][:, 0:96])

    checkpoint("s1")
    Bx.iota(tmpI[:, 0:1], [[0, 1]], 0, 1)
    cp("dve", sm[:, 0:1], tmpI[:, 0:1])
    ts("dve", sm[:, 1:2], sm[:, 0:1], 64.0, None, ALU.is_ge)
    stt(sm[:, 2:3], sm[:, 1:2], -64.0, sm[:, 0:1], ALU.mult, ALU.add)
    Bx.iota(tmpI[:, 16:32], [[2, 16]], 0, 0)
    cp("dve", sm[:, 16:32], tmpI[:, 16:32])
    ts("dve", sm[:, 16:32], sm[:, 16:32], sm[:, 1:2], None, ALU.add)
    Bx.iota(tmpI[:, 32:48], [[1, 16]], 0, 0)
    cp("dve", sm[:, 32:48], tmpI[:, 32:48])
    act(sm[:, 48:64], sm[:, 32:48], AF.Exp, scale=-float(np.log(10000.0)) / 16.0)
    tt("dve", ang[:, :, 0:16], sm[:, 16:32].rearrange("p (t o) -> p t o", o=1).to_broadcast([P, NT, 16]),
       sm[:, 48:64].rearrange("p (o i) -> p o i", o=1).to_broadcast([P, NT, 16]), ALU.mult)
    ts("dve", ang[:, :, 16:32], sm[:, 48:64].rearrange("p (o i) -> p o i", o=1).to_broadcast([P, NT, 16]),
       sm[:, 2:3], None, ALU.mult)
    angf = ang[:].rearrange("p t i -> p (t i)")

    def range_reduce(dst, src, shift):
        ts("dve", tmpA[:], src, 1.0, shift, ALU.mult, ALU.add)
        ts("dve", tmpB[:], tmpA[:], 1.0 / (2 * PI), None, ALU.mult)
        cp("dve", tmpI[:], tmpB[:])
        cp("dve", tmpB[:], tmpI[:])
        stt(tmpC[:], tmpB[:], -2 * PI, tmpA[:], ALU.mult, ALU.add)
        ts("dve", tmpB[:], tmpC[:], PI, None, ALU.is_gt)
        stt(tmpA[:], tmpB[:], -2 * PI, tmpC[:], ALU.mult, ALU.add)
        ts("dve", tmpB[:], tmpA[:], -PI, None, ALU.is_lt)
        stt(tmpC[:], tmpB[:], 2 * PI, tmpA[:], ALU.mult, ALU.add)
        ts("dve", dst, tmpC[:], PI, -PI, ALU.min, ALU.max)

    range_reduce(tmpA[:], angf, 0.0)
    act(sinT[:].rearrange("p t i -> p (t i)"), tmpA[:], AF.Sin)
    ts("dve", nsinT[:].rearrange("p t i -> p (t i)"), sinT[:].rearrange("p t i -> p (t i)"), -1.0, None, ALU.mult)
    range_reduce(tmpA[:], angf, PI / 2)
    act(cosT[:].rearrange("p t i -> p (t i)"), tmpA[:], AF.Sin)

    checkpoint("s2")
    act(sm[:, 0:24], colsT[:, 69:93], AF.Tanh, scale=0.5)
    stt(sm[:, 24:48], sm[:, 0:24], 1.0, colsT[:, 69:93], ALU.add, ALU.mult)
    ts("dve", scT[:].rearrange("p k r -> p r k"), sm[:, 24:48].rearrange("p (r k) -> p r k", k=8), 0.5, None, ALU.mult)

    act(gam[:], dec[:], AF.Tanh, scale=0.5)
    ts("dve", gam[:], gam[:], 1.0, 0.5, ALU.add, ALU.mult)
    for d_ in range(2):
        for pr in range(2):
            cp("dve", gcol[0:64, 2 * d_ + pr:2 * d_ + pr + 1], gam[0:64, 4 * d_ + 2 * pr:4 * d_ + 2 * pr + 1])
            cp("dve", gcol[64:128, 2 * d_ + pr:2 * d_ + pr + 1], gam[64:128, 4 * d_ + 2 * pr + 1:4 * d_ + 2 * pr + 2])
    Bx.iota(iot[:], [[1, P]], 1, 0)
    cp("dve", E1[:], iot[:])
    Bx.iota(iot[:], [[-1, P]], 128, 0)
    cp("dve", E2[:], iot[:])
    Bx.iota(tmpI[:, 0:4], [[0, 4]], 127, -1)
    cp("dve", EK[:, 0:4], tmpI[:, 0:4])
    Bx.iota(tmpI[:, 4:8], [[0, 4]], 0, 1)
    cp("dve", EK[:, 4:8], tmpI[:, 4:8])
    for k in range(4):
        tt("pool", WQ[:, k, :], gcol[:, k:k + 1].to_broadcast([P, P]), E1[:] if k < 2 else E2[:], ALU.pow)
    tt("pool", cdc[:], gcol[:], c128[:], ALU.pow)
    tt("pool", wk[:], gam[:], EK[:], ALU.pow)
    Dp = tmpA[:, 0:P]; Dn = tmpA[:, P:2 * P]; Mp = tmpA[:, 2 * P:3 * P]; Mn = tmpA[:, 3 * P:4 * P]
    ts("dve", Dp, Df[:], 0.0, None, ALU.max)
    ts("dve", Dn, Df[:], -1.0, 0.0, ALU.mult, ALU.max)
    ts("dve", Mp, Df[:], 0.0, None, ALU.is_ge)
    ts("dve", Mn, Df[:], 0.0, None, ALU.is_le)
    for h in range(4):
        ta = tmpB[:, 0:P]; tb = tmpB[:, P:2 * P]
        tt("pool", ta, gam[:, h:h + 1].to_broadcast([P, P]), Dp, ALU.pow)
        tt("pool", tb, gam[:, 4 + h:5 + h].to_broadcast([P, P]), Dn, ALU.pow)
        tt("dve", ta, ta, Mp, ALU.mult)
        tt("dve", tb, tb, Mn, ALU.mult)
        tt("dve", maskT[:, h, :], ta, tb, ALU.add)
    ts("dve", gret_row[:], gret_row[:], 0.5, None, ALU.mult)

    checkpoint("s3")
    modps = banks[1]
    wada_v = wada_d.rearrange("(c p) n -> p c n", p=P)
    for bl in range(6):
        slot = wada_s[bl % 2]
        S.dma("pool", slot[:], wada_v[:, :, bl * 1024:(bl + 1) * 1024])
        for jj in range(8):
            j = bl * 8 + jj
            for kc in range(8):
                mm(modps[:, j * 3:(j + 1) * 3], slot[:, kc, jj * P:(jj + 1) * P], scT[:, kc, :], start=(kc == 0), stop=(kc == 7))
    tt("dve", modT[:], v3(modps[:, 0:144], 48, 3), colsT[:, 0:48].rearrange("p (j o) -> p j o", o=1).to_broadcast([P, 48, 3]), ALU.add)
    for r in range(3):
        stt(Gall[:, r, 0, :], modT[:, 8:16, r], 1.0, colsT[:, 48:56], ALU.add, ALU.mult)
        stt(Gall[:, r, 1, :], modT[:, 32:40, r], 1.0, colsT[:, 56:64], ALU.add, ALU.mult)
    dbg_dump("modT", modT[:], [P, 48, 3])
    dbg_dump("cos", cosT[:], [P, NT, 32])
    dbg_dump("sin", sinT[:], [P, NT, 32])
    dbg_dump("maskT", maskT[:], [P, 4, P])
    dbg_dump("WQ", WQ[:], [P, 4, P])
    dbg_dump("wk", wk[:], [P, 8])
    dbg_dump("cdc", cdc[:], [P, 4])

    def Gc(r, which, c):
        return Gall[:, r, which, c:c + 1]

    def SHc(r, which, c):
        j = (0 if which == 0 else 24) + c
        return modT[:, j, r:r + 1]

    def make_gate_row(dst, r, base):
        for half in range(2):
            bank = banks[2 + half]
            for cc in range(4):
                c = half * 4 + cc
                dgt = gdiag.next()
                ts("dve", dgt, ident_f[:], modT[:, base + c, r:r + 1], None, ALU.mult)
                mm(bank[:, cc * P:(cc + 1) * P], ones_f[:], dgt, start=True, stop=True)
            cp("dve", dst[:, half * 512:(half + 1) * 512], bank[:, :])

    checkpoint("setup")

    def norm_T(L_, src, r, which, hT):
        xt = L_["xt"].next(); xn = L_["xn"].next(); tpb = bfv(L_["tp"].next())
        S.dma("sp", xt[:], src)
        s = st(3)
        act(L_["junk"].next()[:], xt[:], AF.Square, accum=s[:, 0:1])
        ts("dve", s[:, 1:2], s[:, 0:1], 1.0 / D, EPS, ALU.mult, ALU.add)
        tt("pool", s[:, 2:3], s[:, 1:2], mhalf[:, 0:1], ALU.pow)
        ts("dve", xn[:], xt[:], s[:, 2:3], None, ALU.mult)
        for c in range(8):
            tr(tpb[:, c * P:(c + 1) * P], xn[:, c * P:(c + 1) * P], ident_b[:])
        for c in range(8):
            act(hT[:, c, :], tpb[:, c * P:(c + 1) * P], AF.Identity, scale=Gc(r, which, c), bias=SHc(r, which, c))
        return xt

    def proj(hT, w, col0, ncols, bank):
        for c in range(8):
            mm(bank[:, 0:ncols], hT[:, c, :], w[:, c, col0:col0 + ncols], start=(c == 0), stop=(c == 7))

    def rope(L_, ps, nh, t, out):
        tA = L_["ropeA"].next()[:, 0:nh * 64]; tB = L_["ropeB"].next()[:, 0:nh * 64]
        tt("dve", v3(tA, 2 * nh, 32), v3(ps, 2 * nh, 32), cosT[:, t:t + 1, :].to_broadcast([P, 2 * nh, 32]), ALU.mult)
        ps4 = ps.rearrange("p (h two i) -> p h two i", two=2, i=32)
        tB4 = tB.rearrange("p (h two i) -> p h two i", two=2, i=32)
        tt("dve", tB4[:, :, 0, :], ps4[:, :, 1, :], nsinT[:, t:t + 1, :].to_broadcast([P, nh, 32]), ALU.mult)
        tt("dve", tB4[:, :, 1, :], ps4[:, :, 0, :], sinT[:, t:t + 1, :].to_broadcast([P, nh, 32]), ALU.mult)
        tt("pool", out, tA, tB, ALU.add)

    LR = Lay("R")
    R_qkT = LR.a("qkT", [P, 6, L], BF16)
    R_kf = LR.a("kf", [P, NCH, 256], BF16)
    R_kb = LR.a("kb", [P, NCH, 256], BF16)
    R_v = LR.a("v", [P, NCH, 512], BF16)
    R_gate = LR.a("gate", [P, NT, 512], BF16)
    R_Sb = LR.a("Sb", [P, NT, 512], BF16)
    LR2 = Lay("R2", start=LR.off)
    _rxn = [LR.a(f"xn{i}", [P, D], BF16) for i in range(2)]
    R_ = dict(
        xt=Ring([LR.a(f"xt{i}", [P, D], F32) for i in range(2)]),
        xn=Ring(_rxn),
        junk=Ring([_rxn[0], _rxn[1]]),
        ropeA=Ring([LR.a(f"ropeA{i}", [P, 512], F32) for i in range(1)]),
        ropeB=Ring([LR.a(f"ropeB{i}", [P, 512], F32) for i in range(1)]),
        tp=Ring([banks[0], banks[1]]),
    )
    R_hT = Ring([LR.a(f"hT{i}", [P, 8, P], BF16) for i in range(2)])
    R_win = LR.a("win", [P, 8, 1536], BF16)
    R_qktm = Ring([LR.a(f"qktm{i}", [P, 512], BF16) for i in range(2)])
    R_th = Ring([LR.a(f"th{i}", [P, 512], BF16) for i in range(2)])
    R_gt = Ring([LR.a(f"gt{i}", [P, 512], F32) for i in range(1)])
    R_Sf = Ring([LR2.a(f"Sf{i}", [P, 512], BF16) for i in range(2)])
    R_Sst = LR2.a("Sst", [P, 2, 512], F32)
    R_PT = Ring([LR2.a(f"PT{i}", [P, 512], BF16) for i in range(2)])
    R_qfb = Ring([LR2.a(f"qfb{i}", [P, 2, 4, P], BF16) for i in range(2)])
    R_yn = Ring([LR2.a(f"yn{i}", [P, 512], F32) for i in range(2)])
    R_ytm = Ring([LR2.a(f"ytm{i}", [P, 512], BF16) for i in range(2)])
    print("R layout bytes", LR.off - ARENA, "of", ARENA_END - ARENA)

    LM = Lay("M")
    M_ = dict(
        xt=Ring([LM.a(f"xt{i}", [P, D], F32) for i in range(2)]),
        xn=Ring([LM.a(f"xn{i}", [P, D], BF16) for i in range(2)]),
        junk=None,
        ropeA=Ring([LM.a(f"ropeA{i}", [P, 256], F32) for i in range(2)]),
        ropeB=Ring([LM.a(f"ropeB{i}", [P, 256], F32) for i in range(2)]),
        tp=Ring([banks[0], banks[1]]),
    )
    M_["junk"] = Ring([M_["xn"].items[0], M_["xn"].items[1]])
    M_junk = LM.a("junkM", [P, 512], BF16)
    M_hT = Ring([LM.a(f"hT{i}", [P, 8, P], BF16) for i in range(2)])
    M_win = LM.a("win", [P, 8, 704], BF16)
    M_wuqn = LM.a("wuqn", [P, 3, 4, P], BF16)
    M_wuqp = LM.a("wuqp", [P, 3, 4, 64], BF16)
    M_wukvn = LM.a("wukvn", [P, 2, 4, P], BF16)
    M_wukvv = LM.a("wukvv", [P, 2, 4, P], BF16)
    M_cn = Ring([LM.a(f"cn{i}", [P, 640], BF16) for i in range(2)])
    M_kpe2 = Ring([LM.a(f"kpe{i}", [P, 2, 64], BF16) for i in range(2)])
    M_cqT = Ring([LM.a(f"cqT{i}", [P, 3, 512], BF16) for i in range(2)])
    M_ckvT = Ring([LM.a(f"ckvT{i}", [P, 2, 512], BF16) for i in range(2)])
    M_qpe = Ring([LM.a(f"qpe{i}", [P, 256], BF16) for i in range(2)])
    M_QnT = LM.a("QnT", [P, 4, L], BF16)
    M_QpT = LM.a("QpT", [P, 4, L], BF16)
    M_KpT = LM.a("KpT", [P, NCH * P], BF16)
    M_KnT = LM.a("KnT", [P, 4, NCH * P], BF16)
    M_Vm = LM.a("Vm", [P, NCH, 512], BF16)
    M_PT = Ring([LM.a(f"PT{i}", [P, 512], BF16) for i in range(3)])
    M_rden = Ring([LM.a(f"rden{i}", [P, 512], F32) for i in range(2)])
    print("M layout bytes", LM.off - ARENA, "of", ARENA_END - ARENA)

    LB = Lay("B")
    B_xm = LB.a("xm", [P, 8, D], F32)
    B_wout = LB.a("wout", [P, 8, D], BF16)
    B_wf1 = [LB.a(f"wf1_{i}", [P, 8, 1024], BF16) for i in range(2)]
    B_wf2 = [LB.a(f"wf2_{i}", [P, 8, 1024], BF16) for i in range(2)]
    B_aT = Ring([LB.a(f"aT{i}", [P, 8, 512], BF16) for i in range(2)])
    _bxn = [LB.a(f"xn{i}", [P, D], BF16) for i in range(2)]
    B_xn = Ring(_bxn)
    B_junk = _bxn[0]
    B_r = Ring([LB.a(f"r{i}", [P, 512], BF16) for i in range(2)])
    B_tmp = Ring([LB.a(f"tmp{i}", [P, 512], F32) for i in range(2)])
    print("B layout bytes", LB.off - ARENA, "of", ARENA_END - ARENA)

    win_v = win_d.rearrange("(c p) n -> p c n", p=P)
    INV_SQRT_DQK = 1.0 / float(np.sqrt(192.0))

    for b in range(nb):
        make_gate_row(GTa, b, 16)
        make_gate_row(GTf, b, 40)

        S.dma("pool", R_win[:], win_v[:, :, 0:1536])
        Bx.memset("dve", R_qkT[:, 0:4, :], 0.0)
        ts("dve", R_win[:, :, 256:512], R_win[:, :, 256:512], 0.125, None, ALU.mult)
        pbank = Ring([banks[2], banks[3], banks[4]])
        for idx in range(NCH):
            lat = idx >= 2
            t = idx - 2
            src = x_d[b, t * P:(t + 1) * P, :] if lat else ctx_d[b, idx * P:(idx + 1) * P, :]
            hT = R_hT.next()
            norm_T(R_, src, b if lat else 2, 0, hT)
            bA = pbank.next()
            if lat:
                proj(hT, R_win, 0, 512, bA)
                qk = R_qktm.next()
                rope(R_, bA[:, 0:512], 8, t, qk[:])
                Tb = bfv(banks[5])
                for k in range(4):
                    tr(Tb[:, k * P:(k + 1) * P], qk[:, k * P:(k + 1) * P], ident_b[:])
                Tb4 = v3(Tb[:, 0:512], 4, P)
                tk = slice(t * P, (t + 1) * P)
                qz = R_qkT[:, 0:4, :].rearrange("p (pr hb) n -> p hb pr n", hb=2)
                cp("act", qz[0:64, 0, :, tk], Tb4[0:64, 0:2, :])
                cp("act", qz[64:128, 1, :, tk], Tb4[64:128, 0:2, :])
                cp("act", R_qkT[:, 4:6, tk], Tb4[:, 2:4, :])
                ksrc = qk[:, 256:512]
                keng = "pool"
            else:
                proj(hT, R_win, 256, 256, bA)
                ksrc = bA[:, 0:256]
                keng = "dve"
            tt(keng, v3(R_kf[:, idx, :], 4, 64), v3(ksrc, 4, 64), wk[:, 0:4].rearrange("p (h o) -> p h o", o=1).to_broadcast([P, 4, 64]), ALU.mult)
            tt(keng, v3(R_kb[:, idx, :], 4, 64), v3(ksrc, 4, 64), wk[:, 4:8].rearrange("p (h o) -> p h o", o=1).to_broadcast([P, 4, 64]), ALU.mult)
            bB = pbank.next()
            proj(hT, R_win, 512, 512, bB)
            cp("act", R_v[:, idx, :], bB[:, :])
            if lat:
                bC = pbank.next()
                proj(hT, R_win, 1024, 512, bC)
                th = R_th.next(); gt = R_gt.next()
                act(th[:], bC[:, :], AF.Tanh, scale=0.5)
                stt(gt[:], th[:], 1.0, bC[:, :], ALU.add, ALU.mult)
                tt("pool", R_gate[:, t, :], gt[:], gret_row[:], ALU.mult)
            checkpoint(f"r_t{idx}")
        if b == 0:
            dbg_dump("qkT", R_qkT[:], [P, 6, L], BF16)
            dbg_dump("v", R_v[:], [P, NCH, 512], BF16)
            dbg_dump("kf", R_kf[:], [P, NCH, 256], BF16)
            dbg_dump("gate", R_gate[:], [P, NT, 512], BF16)

        checkpoint("r1")
        kvbank = Ring([banks[0], banks[1]])
        Bx.memset("dve", R_Sst[:], 0.0)

        def kv_update(idx, d_):
            kb_ = kvbank.next()
            ksrc = R_kf if d_ == 0 else R_kb
            for pr in range(2):
                mm(kb_[:, pr * 256:(pr + 1) * 256], ksrc[:, idx, pr * P:(pr + 1) * P], R_v[:, idx, pr * 256:(pr + 1) * 256])
            for pr in range(2):
                stt(R_Sst[:, d_, pr * 256:(pr + 1) * 256], R_Sst[:, d_, pr * 256:(pr + 1) * 256],
                    cdc[:, 2 * d_ + pr:2 * d_ + pr + 1], kb_[:, pr * 256:(pr + 1) * 256], ALU.mult, ALU.add)

        kv_update(1, 1)
        kv_update(0, 1)
        for n in range(NT - 1, -1, -1):
            cp("dve", R_Sb[:, n, :], R_Sst[:, 1, :])
            if n > 0:
                kv_update(n + 2, 1)
        checkpoint("r2")
        kv_update(0, 0)
        kv_update(1, 0)
        scb = Ring([banks[2], banks[3]]); ob = Ring([banks[4], banks[5]]); Tbk = Ring([banks[6], banks[7]])
        for n in range(NT):
            Sf = R_Sf.next()
            cp("dve", Sf[:], R_Sst[:, 0, :])
            tok = slice(n * P, (n + 1) * P)
            sc = scb.next()
            for h in range(4):
                pr, hb = h // 2, h % 2
                mm(sc[:, h * P:(h + 1) * P], R_qkT[:, 4 + pr, tok], R_qkT[:, h, tok])
            PT = R_PT.next()
            tt("dve", PT[:], sc[:, :], maskT[:].rearrange("p h i -> p (h i)"), ALU.mult)
            if n == 0:
                checkpoint("r4")
            qfb = R_qfb.next()
            for d_ in range(2):
                tt("pool", qfb[:, d_, :, :].rearrange("p (pr hb) i -> p pr hb i", hb=2),
                   R_qkT[:, 0:4, tok].rearrange("p (pr hb) i -> p pr hb i", hb=2),
                   WQ[:, 2 * d_:2 * d_ + 2, :].rearrange("p (pr o) i -> p pr o i", o=1).to_broadcast([P, 2, 2, P]), ALU.mult)
            o = ob.next()
            for h in range(4):
                pr, hb = h // 2, h % 2
                ps_ = slice(hb * 64, (hb + 1) * 64)
                dv = slice(pr * 256 + hb * P, pr * 256 + (hb + 1) * P)
                oo = o[:, h * P:(h + 1) * P]
                mm(oo, PT[:, h * P:(h + 1) * P], R_v[:, n + 2, h * P:(h + 1) * P], start=True, stop=False)
                mm(oo, qfb[:, 0, h, :], Sf[:, dv], start=False, stop=False)
                mm(oo, qfb[:, 1, h, :], R_Sb[:, n, dv], start=False, stop=True)
            if n == 0:
                checkpoint("r5")
            s6 = st(24); mv = st(8); s4 = st(12)
            for h in range(4):
                S.op("dve", lambda e, h=h, o=o, s6=s6: e.bn_stats(s6[:, h * 6:(h + 1) * 6], o[:, h * P:(h + 1) * P]),
                     r=[o[:, h * P:(h + 1) * P]], w=[s6[:, h * 6:(h + 1) * 6]])
            for h in range(4):
                S.op("dve", lambda e, h=h, mv=mv, s6=s6: e.bn_aggr(mv[:, h * 2:(h + 1) * 2], s6[:, h * 6:(h + 1) * 6]),
                     r=[s6[:, h * 6:(h + 1) * 6]], w=[mv[:, h * 2:(h + 1) * 2]])
            mv2 = mv.rearrange("p (h two) -> p h two", two=2)
            ts("dve", s4[:, 0:4], mv2[:, :, 1], EPS, None, ALU.add)
            tt("pool", s4[:, 4:8], s4[:, 0:4], mhalf[:], ALU.pow)
            stt(s4[:, 8:12], mv2[:, :, 0], -1.0, s4[:, 4:8], ALU.mult, ALU.mult)
            if n == 0:
                checkpoint("r6")
            yn = R_yn.next()
            for h in range(4):
                act(yn[:, h * P:(h + 1) * P], o[:, h * P:(h + 1) * P], AF.Identity, scale=s4[:, 4 + h:5 + h], bias=s4[:, 8 + h:9 + h])
            ytm = R_ytm.next()
            tt("dve", ytm[:], yn[:], R_gate[:, n, :], ALU.mult)
            Tb = bfv(Tbk.next())
            for h in range(4):
                tr(Tb[:, h * P:(h + 1) * P], ytm[:, h * P:(h + 1) * P], ident_b[:])
            cp("act", YT[:, 0:4, tok], v3(Tb[:, 0:512], 4, P))
            if n == 0:
                checkpoint("r8")
            if n < NT - 1:
                kv_update(n + 2, 0)
        if b == 0:
            dbg_dump("YTret", YT[:, 0:4, :], [P, 4, L], BF16)
        checkpoint("ret")

        S.dma("pool", M_win[:], win_v[:, :, 1536:2240])
        Bx.memset("dve", M_QpT[:], 0.0)
        wuq_v = wuq_d.rearrange("(c p) (h d) -> p c h d", p=P, d=192)
        for c in range(3):
            S.dma("pool", M_wuqn[:, c, :, :], wuq_v[:, c, :, 0:128])
            S.dma("pool", M_wuqp[:, c, :, :], wuq_v[:, c, :, 128:192])
        wukv_v = wukv_d.rearrange("(c p) (h d) -> p c h d", p=P, d=256)
        for c in range(2):
            S.dma("pool", M_wukvn[:, c, :, :], wukv_v[:, c, :, 0:128])
            S.dma("pool", M_wukvv[:, c, :, :], wukv_v[:, c, :, 128:256])
        pbank = Ring([banks[2], banks[3]])
        ubank = Ring([banks[5], banks[6]])
        groups = [[0, 1]] + [[2 + 4 * g + i for i in range(4)] for g in range(4)]
        for gi, gtiles in enumerate(groups):
            latg = gi > 0
            N = len(gtiles) * P
            cqT = M_cqT.next(); ckvT = M_ckvT.next()
            for ti, idx in enumerate(gtiles):
                t = idx - 2
                src = x_d[b, t * P:(t + 1) * P, :] if latg else ctx_d[b, idx * P:(idx + 1) * P, :]
                hT = M_hT.next()
                norm_T(M_, src, b if latg else 2, 0, hT)
                cn = M_cn.next()
                tpos = slice(ti * P, (ti + 1) * P)
                if latg:
                    bD = pbank.next()
                    proj(hT, M_win, 0, 384, bD)
                    s = st(3)
                    act(M_junk[:, 0:384], bD[:, 0:384], AF.Square, accum=s[:, 0:1])
                    ts("dve", s[:, 1:2], s[:, 0:1], 1.0 / 384, EPS, ALU.mult, ALU.add)
                    tt("pool", s[:, 2:3], s[:, 1:2], mhalf[:, 0:1], ALU.pow)
                    stt(cn[:, 0:384], bD[:, 0:384], s[:, 2:3], gq_row[:], ALU.mult, ALU.mult)
                bE = pbank.next()
                proj(hT, M_win, 384, 320, bE)
                s = st(3)
                act(M_junk[:, 0:256], bE[:, 0:256], AF.Square, accum=s[:, 0:1])
                ts("dve", s[:, 1:2], s[:, 0:1], 1.0 / 256, EPS, ALU.mult, ALU.add)
                tt("pool", s[:, 2:3], s[:, 1:2], mhalf[:, 0:1], ALU.pow)
                stt(cn[:, 384:640], bE[:, 0:256], s[:, 2:3], gkv_row[:], ALU.mult, ALU.mult)
                kpe = M_kpe2.next()
                if latg:
                    rope(M_, bE[:, 256:320], 1, t, kpe[:, 0, :])
                else:
                    cp("dve", kpe[:, 0, :], bE[:, 256:320])
                cp("pool", kpe[:, 1, :], kpe[:, 0, :])
                Tb = bfv(banks[4])
                if latg:
                    for c in range(3):
                        tr(Tb[:, c * P:(c + 1) * P], cn[:, c * P:(c + 1) * P], ident_b[:])
                for c in range(2):
                    tr(Tb[:, (3 + c) * P:(4 + c) * P], cn[:, 384 + c * P:384 + (c + 1) * P], ident_b[:])
                tr(Tb[:, 5 * P:6 * P], kpe[:].rearrange("p a d -> p (a d)"), ident_b[:])
                if latg:
                    cp("act", cqT[:, :, tpos], v3(Tb[:, 0:384], 3, P))
                cp("act", ckvT[:, :, tpos], v3(Tb[:, 384:640], 2, P))
                cp("dve", M_KpT[:, idx * P:(idx + 1) * P], Tb[:, 5 * P:6 * P])
                if latg:
                    bU = ubank.next()
                    for c in range(3):
                        mm(bU[:, 0:256], cqT[:, c, tpos], M_wuqp[:, c, :, :].rearrange("p h d -> p (h d)"), start=(c == 0), stop=(c == 2))
                    qpe = M_qpe.next()
                    rope(M_, bU[:, 0:256], 4, t, qpe[:])
                    Tb2 = bfv(banks[7])
                    for pr in range(2):
                        tr(Tb2[:, pr * P:(pr + 1) * P], qpe[:, pr * P:(pr + 1) * P], ident_b[:])
                    qpz = M_QpT[:].rearrange("p (pr hb) n -> p hb pr n", hb=2)
                    Tb2v = v3(Tb2[:, 0:256], 2, P)
                    cp("dve", qpz[0:64, 0, :, t * P:(t + 1) * P], Tb2v[0:64, :, :])
                    cp("dve", qpz[64:128, 1, :, t * P:(t + 1) * P], Tb2v[64:128, :, :])
                bV = ubank.next()
                for c in range(2):
                    mm(bV[:, :], ckvT[:, c, tpos], M_wukvv[:, c, :, :].rearrange("p h d -> p (h d)"), start=(c == 0), stop=(c == 1))
                cp("act", M_Vm[:, idx, :], bV[:, :])
            key0 = gtiles[0] * P
            for h in range(4):
                if latg:
                    bW = ubank.next()
                    for c in range(3):
                        mm(bW[:, 0:N], M_wuqn[:, c, h, :], cqT[:, c, 0:N], start=(c == 0), stop=(c == 2))
                    t0 = (gtiles[0] - 2) * P
                    cp("dve", M_QnT[:, h, t0:t0 + N], bW[:, 0:N])
                bW = ubank.next()
                for c in range(2):
                    mm(bW[:, 0:N], M_wukvn[:, c, h, :], ckvT[:, c, 0:N], start=(c == 0), stop=(c == 1))
                cp("act", M_KnT[:, h, key0:key0 + N], bW[:, 0:N])
        if b == 0:
            dbg_dump("QnT", M_QnT[:], [P, 4, L], BF16)
            dbg_dump("QpT", M_QpT[:], [P, 4, L], BF16)
            dbg_dump("KpT", M_KpT[:], [P, NCH * P], BF16)
            dbg_dump("KnT", M_KnT[:], [P, 4, NCH * P], BF16)
            dbg_dump("Vm", M_Vm[:], [P, NCH, 512], BF16)

        sbank = Ring([banks[0], banks[1], banks[2]])
        obank = Ring([banks[3], banks[4]]); dbank = Ring([banks[5], banks[6]])
        steps = [(h, qg, kt) for h in range(4) for qg in range(4) for kt in range(NCH)]
        sT = {}

        def qk_step(i):
            h, qg, kt = steps[i]
            pr, hb = h // 2, h % 2
            bank = sbank.next()
            sT[i] = bank
            qs = slice(qg * 512, (qg + 1) * 512)
            ks = slice(kt * P, (kt + 1) * P)
            mm(bank[:, :], M_KnT[:, h, ks], M_QnT[:, h, qs], start=True, stop=False)
            mm(bank[:, :], M_KpT[:, ks], M_QpT[:, h, qs], start=False, stop=True)

        qk_step(0)
        qk_step(1)
        cur_o = cur_d = None
        for i, (h, qg, kt) in enumerate(steps):
            if kt == 0:
                cur_o = obank.next(); cur_d = dbank.next()
            PT = M_PT.next()
            act(PT[:], sT.pop(i)[:, :], AF.Exp, scale=INV_SQRT_DQK)
            if i + 2 < len(steps):
                qk_step(i + 2)
            mm(cur_o[:, :], M_Vm[:, kt, h * P:(h + 1) * P], PT[:], start=(kt == 0), stop=(kt == NCH - 1))
            mm(cur_d[:, :], ones_b[:], PT[:], start=(kt == 0), stop=(kt == NCH - 1))
            if kt == NCH - 1:
                rd = M_rden.next()
                S.op("dve", lambda e, rd=rd, cur_d=cur_d: e.reciprocal(rd[:], cur_d[:, :]), r=[cur_d[:, :]], w=[rd[:]])
                tt("dve", YT[:, 4 + h, qg * 512:(qg + 1) * 512], cur_o[:, :], rd[:], ALU.mult)
        if b == 0:
            dbg_dump("YT", YT[:], [P, 8, L], BF16)
        checkpoint("mla")

        S.dma("pool", B_wout[:], wout_d.rearrange("(c p) n -> p c n", p=P))
        wff1_v = wff1_d.rearrange("(c p) n -> p c n", p=P)

        def load_ffq(q, slot):
            S.dma("pool", B_wf1[slot][:], wff1_v[:, :, q * 1024:(q + 1) * 1024])
            S.dma("pool", B_wf2[slot][:], wff2_d[q * 1024:(q + 1) * 1024, :].rearrange("(c p) n -> p c n", p=P))

        for hs in range(2):
            if hs == 0:
                load_ffq(0, 0)
                load_ffq(1, 1)
            wob = Ring([banks[0], banks[1]]); tpb_ = Ring([banks[2], banks[3]])
            for tl in range(8):
                t = hs * 8 + tl
                tok = slice(t * P, (t + 1) * P)
                S.dma("sp", B_xm[:, tl, :], x_d[b, tok, :])
                for half in range(2):
                    bk = wob.next()
                    for c in range(8):
                        mm(bk[:, :], YT[:, c, tok], B_wout[:, c, half * 512:(half + 1) * 512], start=(c == 0), stop=(c == 7))
                    tmp = B_tmp.next()
                    tt("dve", tmp[:], bk[:, :], GTa[:, half * 512:(half + 1) * 512], ALU.mult)
                    tt("pool", B_xm[:, tl, half * 512:(half + 1) * 512], B_xm[:, tl, half * 512:(half + 1) * 512], tmp[:], ALU.add)
                xn = B_xn.next(); tpv = bfv(tpb_.next())
                s = st(3)
                act(B_junk[:], B_xm[:, tl, :], AF.Square, accum=s[:, 0:1])
                ts("dve", s[:, 1:2], s[:, 0:1], 1.0 / D, EPS, ALU.mult, ALU.add)
                tt("pool", s[:, 2:3], s[:, 1:2], mhalf[:, 0:1], ALU.pow)
                ts("dve", xn[:], B_xm[:, tl, :], s[:, 2:3], None, ALU.mult)
                for c in range(8):
                    tr(tpv[:, c * P:(c + 1) * P], xn[:, c * P:(c + 1) * P], ident_b[:])
                for c in range(8):
                    act(YT[:, c, tok], tpv[:, c * P:(c + 1) * P], AF.Identity, scale=Gc(b, 1, c), bias=SHc(b, 1, c))
            if b == 0 and hs == 0:
                dbg_dump("xmid", B_xm[:], [P, 8, D])
            f1b = Ring([banks[4], banks[5]]); f2b = Ring([banks[6], banks[7]])
            for q in range(4):
                slot = q % 2
                for g in range(2):
                    gs = slice(hs * 1024 + g * 512, hs * 1024 + (g + 1) * 512)
                    aT = B_aT.next()
                    for j in range(8):
                        bk = f1b.next()
                        for c in range(8):
                            mm(bk[:, :], B_wf1[slot][:, c, j * P:(j + 1) * P], YT[:, c, gs], start=(c == 0), stop=(c == 7))
                        r_ = B_r.next()
                        act(r_[:], bk[:, :], AF.Relu)
                        tt("dve", aT[:, j, :], r_[:], r_[:], ALU.mult)
                    for tl4 in range(4):
                        tl = g * 4 + tl4
                        for half in range(2):
                            bk = f2b.next()
                            for j in range(8):
                                mm(bk[:, :], aT[:, j, tl4 * P:(tl4 + 1) * P], B_wf2[slot][:, j, half * 512:(half + 1) * 512], start=(j == 0), stop=(j == 7))
                            tmp = B_tmp.next()
                            tt("dve", tmp[:], bk[:, :], GTf[:, half * 512:(half + 1) * 512], ALU.mult)
                            tt("pool", B_xm[:, tl, half * 512:(half + 1) * 512], B_xm[:, tl, half * 512:(half + 1) * 512], tmp[:], ALU.add)
                if q + 2 < 4:
                    load_ffq(q + 2, slot)
                elif hs == 0:
                    load_ffq(q - 2, slot)
            for tl in range(8):
                t = hs * 8 + tl
                s = st(3)
                act(B_junk[:], B_xm[:, tl, :], AF.Square, accum=s[:, 0:1])
                ts("dve", s[:, 1:2], s[:, 0:1], 1.0 / D, EPS, ALU.mult, ALU.add)
                tt("pool", s[:, 2:3], s[:, 1:2], mhalf[:, 0:1], ALU.pow)
                stt(B_xm[:, tl, :], B_xm[:, tl, :], s[:, 2:3], gfin_row[:], ALU.mult, ALU.mult)
                S.dma("sp", out_d[b, t * P:(t + 1) * P, :], B_xm[:, tl, :], out=True)

    S.emit()
    print("sched stats (ops, waits):", S.stats)


_NAMES = ["x", "c", "ctx", "c_ctx", "w_ada", "b_ada", "g_attn", "g_ffn", "w_in", "ret_decay_fwd", "ret_decay_bwd",
          "g_ret", "g_q_lora", "w_uq", "g_kv_lora", "w_ukv", "w_out", "w_ff1", "w_ff2", "g_final"]
_PER_BATCH = ("x", "c", "ctx")
_DEPTH_AXIS = ("w_ada", "b_ada", "g_attn", "g_ffn", "w_in", "ret_decay_fwd", "ret_decay_bwd", "g_ret", "g_q_lora",
               "w_uq", "g_kv_lora", "w_ukv", "w_out", "w_ff1", "w_ff2")


def make_in_maps(inputs, n_cores, nb):
    shared = {}
    for k in _NAMES:
        a = np.ascontiguousarray(np.asarray(inputs[k], dtype=np.float32))
        if k in _DEPTH_AXIS:
            a = a[0]
        shared[k] = a
    maps = []
    for i in range(n_cores):
        m = {}
        for k in _NAMES:
            if k in _PER_BATCH:
                m[k] = np.ascontiguousarray(shared[k][i * nb:(i + 1) * nb])
            else:
                m[k] = shared[k]
        maps.append(m)
    return maps


def kernel(**inputs):
    n_cores, nb = 8, 2
    nc, _ = build(nb=nb)
    in_maps = make_in_maps(inputs, n_cores, nb)
    res = run_bass_kernel_spmd(nc, in_maps, core_ids=list(range(n_cores)))
    return np.concatenate([np.asarray(r["out"]) for r in res.results], axis=0).astype(np.float32)
```
